# Optimizing a Trainium2 kernel written in Bass

```python
import jax, jax.numpy as jnp
from jax import lax
import numpy as np

D_MODEL = 1024
BATCH = 1
SEQ = 16384
DEPTH = 4

CHUNK = 64
HEAD_DIM = 64
N_HEADS_SB = 4
N_HEADS_CH = 8
CONV_CH = 256
CONV_WIDTH = 3
N_PREV_CHUNKS = 8
BAND = (N_PREV_CHUNKS + 1) * CHUNK
REL_CLIP = 128
Q_BLOCK = 128
D_SB = N_HEADS_SB * HEAD_DIM
D_CH = N_HEADS_CH * HEAD_DIM
D_MIX = D_SB + D_CH + CONV_CH
D_IN = 3 * D_SB + 3 * D_CH + 3 * CONV_CH
N_OUT_GROUPS = D_MIX // HEAD_DIM
D_FF = ((8 * D_MODEL // 3 + 255) // 256) * 256
EPS = 1e-6

kernel_name = "hybrid_stickbreak_chunkattn_shortconv_block"


def rmsnorm(x, w):
    xf = x.astype(jnp.float32)
    y = xf * lax.rsqrt(jnp.mean(xf * xf, axis=-1, keepdims=True) + EPS)
    return (y * w.astype(jnp.float32)).astype(x.dtype)


def group_rmsnorm(y, w):
    b, s, _ = y.shape
    yg = y.astype(jnp.float32).reshape(b, s, N_OUT_GROUPS, HEAD_DIM)
    yg = yg * lax.rsqrt(jnp.mean(yg * yg, axis=-1, keepdims=True) + EPS)
    return (yg.reshape(b, s, D_MIX) * w.astype(jnp.float32)).astype(y.dtype)


def to_heads(t):
    b, s, _ = t.shape
    return t.reshape(b, s, -1, HEAD_DIM).transpose(0, 2, 1, 3)


def from_heads(t):
    b, h, s, d = t.shape
    return t.transpose(0, 2, 1, 3).reshape(b, s, h * d)


def stick_breaking_attention(q, k, v):
    b, h, s, d = q.shape
    n = s // Q_BLOCK
    qb = (q.astype(jnp.float32) * (d ** -0.5)).reshape(b, h, n, Q_BLOCK, d)
    kb = k.astype(jnp.float32).reshape(b, h, n, Q_BLOCK, d)
    vb = v.astype(jnp.float32).reshape(b, h, n, Q_BLOCK, d)
    idx = jnp.arange(Q_BLOCK)
    after_mat = (idx[:, None] > idx[None, :]).astype(jnp.float32)
    diag_mask = idx[None, :] < idx[:, None]
    out = jnp.zeros((b, h, n, Q_BLOCK, d), jnp.float32)
    carry = jnp.zeros((b, h, n, Q_BLOCK), jnp.float32)
    for o in range(n):
        z = jnp.einsum('bhnqd,bhnkd->bhnqk', qb[:, :, o:], kb[:, :, :n - o])
        log_rest = jax.nn.log_sigmoid(-z)
        if o == 0:
            log_rest = jnp.where(diag_mask, log_rest, 0.0)
        after = jnp.einsum('bhnqj,js->bhnqs', log_rest, after_mat) + carry[:, :, o:, :, None]
        w = jnp.exp(jax.nn.log_sigmoid(z) + after)
        if o == 0:
            w = jnp.where(diag_mask, w, 0.0)
        out = out.at[:, :, o:].add(jnp.einsum('bhnqk,bhnkd->bhnqd', w, vb[:, :, :n - o]))
        carry = carry.at[:, :, o:].add(jnp.sum(log_rest, axis=-1))
    return out.reshape(b, h, s, d).astype(v.dtype)


def chunked_relpos_attention(q, k, v, rel_bias):
    b, h, s, d = q.shape
    nc = s // CHUNK
    pad = N_PREV_CHUNKS * CHUNK
    qc = (q.astype(jnp.float32) * (d ** -0.5)).reshape(b, h, nc, CHUNK, d)
    kpad = jnp.pad(k, ((0, 0), (0, 0), (pad, 0), (0, 0))).reshape(b, h, nc + N_PREV_CHUNKS, CHUNK, d)
    vpad = jnp.pad(v, ((0, 0), (0, 0), (pad, 0), (0, 0))).reshape(b, h, nc + N_PREV_CHUNKS, CHUNK, d)
    band_idx = jnp.arange(nc)[:, None] + jnp.arange(N_PREV_CHUNKS + 1)[None, :]
    kb = kpad[:, :, band_idx].reshape(b, h, nc, BAND, d).astype(jnp.float32)
    vb = vpad[:, :, band_idx].reshape(b, h, nc, BAND, d).astype(jnp.float32)
    scores = jnp.einsum('bhcqd,bhckd->bhcqk', qc, kb)
    p = jnp.arange(CHUNK)[:, None]
    m = jnp.arange(BAND)[None, :]
    rel = jnp.clip(N_PREV_CHUNKS * CHUNK + p - m, -REL_CLIP, REL_CLIP) + REL_CLIP
    bias = rel_bias.astype(jnp.float32)[:, rel]
    valid = jnp.repeat(band_idx >= N_PREV_CHUNKS, CHUNK, axis=1)
    scores = jnp.where(valid[None, None, :, None, :], scores + bias[None, :, None], -jnp.inf)
    probs = jax.nn.softmax(scores, axis=-1)
    out = jnp.einsum('bhcqk,bhckd->bhcqd', probs, vb)
    return out.reshape(b, h, s, d).astype(v.dtype)


def short_conv_mixer(gate_b, gate_c, xc, conv_w):
    hc = gate_c * xc
    y = lax.conv_general_dilated(
        hc, conv_w[:, None, :].astype(hc.dtype), window_strides=(1,),
        padding=[(CONV_WIDTH - 1, 0)], dimension_numbers=('NWC', 'WIO', 'NWC'),
        feature_group_count=CONV_CH)
    return gate_b * y


def swiglu(x, w_gate, w_up, w_down):
    return (jax.nn.silu(x @ w_gate) * (x @ w_up)) @ w_down


def setup_inputs(seed: int = 0) -> dict:
    key = jax.random.key(seed)
    ks = jax.random.split(key, 14)
    f32 = jnp.float32

    def gain(k, n):
        return 1.0 + 0.02 * jax.random.normal(k, (DEPTH, n), f32)

    return {
        "x": jax.random.normal(ks[0], (BATCH, SEQ, D_MODEL), f32),
        "attn_norm_w": gain(ks[1], D_MODEL),
        "w_in": jax.random.normal(ks[2], (DEPTH, D_MODEL, D_IN), f32) * D_MODEL ** -0.5,
        "q_norm_w": gain(ks[3], HEAD_DIM),
        "k_norm_w": gain(ks[4], HEAD_DIM),
        "rel_bias": 0.1 * jax.random.normal(ks[5], (DEPTH, N_HEADS_CH, 2 * REL_CLIP + 1), f32),
        "conv_w": jax.random.normal(ks[6], (DEPTH, CONV_WIDTH, CONV_CH), f32) * CONV_WIDTH ** -0.5,
        "out_norm_w": gain(ks[7], D_MIX),
        "w_out": jax.random.normal(ks[8], (DEPTH, D_MIX, D_MODEL), f32) * D_MIX ** -0.5,
        "ffn_norm_w": gain(ks[9], D_MODEL),
        "w_gate": jax.random.normal(ks[10], (DEPTH, D_MODEL, D_FF), f32) * D_MODEL ** -0.5,
        "w_up": jax.random.normal(ks[11], (DEPTH, D_MODEL, D_FF), f32) * D_MODEL ** -0.5,
        "w_down": jax.random.normal(ks[12], (DEPTH, D_FF, D_MODEL), f32) * D_FF ** -0.5,
    }


def reference(x, attn_norm_w, w_in, q_norm_w, k_norm_w, rel_bias, conv_w, out_norm_w,
              w_out, ffn_norm_w, w_gate, w_up, w_down):
    widths = [D_SB] * 3 + [D_CH] * 3 + [CONV_CH] * 3
    split_points = [int(v) for v in np.cumsum(widths)[:-1]]
    for l in range(DEPTH):
        h = rmsnorm(x, attn_norm_w[l])
        proj = h @ w_in[l]
        (q_a, k_a, v_a, q_b, k_b, v_b, g_b, g_c, x_c) = jnp.split(proj, split_points, axis=-1)
        y_sb = from_heads(stick_breaking_attention(to_heads(q_a), to_heads(k_a), to_heads(v_a)))
        qh = rmsnorm(to_heads(q_b), q_norm_w[l])
        kh = rmsnorm(to_heads(k_b), k_norm_w[l])
        y_ch = from_heads(chunked_relpos_attention(qh, kh, to_heads(v_b), rel_bias[l]))
        y_cv = short_conv_mixer(g_b, g_c, x_c, conv_w[l])
        y = group_rmsnorm(jnp.concatenate([y_sb, y_ch, y_cv], axis=-1), out_norm_w[l])
        x = x + y @ w_out[l]
        x = x + swiglu(rmsnorm(x, ffn_norm_w[l]), w_gate[l], w_up[l], w_down[l])
    return x
```

```python
import math
import numpy as np
import ml_dtypes
import concourse.bass as bass
import concourse.mybir as mybir
from contextlib import ExitStack
from concourse.bass_utils import run_bass_kernel_spmd


F32 = mybir.dt.float32
BF16 = mybir.dt.bfloat16
I32 = mybir.dt.int32
AF = mybir.ActivationFunctionType
ALU = mybir.AluOpType
AX = mybir.AxisListType


class Buf:
    __slots__ = ("name", "w", "r", "sem", "semv")

    def __init__(self, name):
        self.name = name
        self.w = None
        self.r = {}
        self.sem = None
        self.semv = 0


class Prog:
    ENG = ("pe", "act", "dve", "pool", "sp")

    def __init__(self, nc, es):
        self.nc = nc
        self.es = es
        self.E = {"pe": nc.tensor, "act": nc.scalar, "dve": nc.vector,
                  "pool": nc.gpsimd, "sp": nc.sync}
        self.sems = {}
        self.cnt = {}
        for e in self.ENG:
            self.sems[e] = es.enter_context(nc.semaphore("sem_" + e))
            self.cnt[e] = 0
        self.seen = {}
        self.nbuf = 0

    def buf(self, name=None):
        self.nbuf += 1
        return Buf(name or f"b{self.nbuf}")

    def bufs(self, n, name="b"):
        return [self.buf(f"{name}{i}") for i in range(n)]

    def sbuf(self, name, shape, dt):
        return self.es.enter_context(self.nc.sbuf_tensor(name, shape, dt))

    def psum(self, name, shape, dt):
        return self.es.enter_context(self.nc.psum_tensor(name, shape, dt))

    def _need(self, reads, writes):
        toks = {}
        def add(t):
            if t is None:
                return
            k, v = t
            if toks.get(k, 0) < v:
                toks[k] = v
        for b in reads:
            add(b.w)
        for b in writes:
            add(b.w)
            for k, v in b.r.items():
                add((k, v))
        return toks

    def _do_waits(self, eng, toks):
        pend = []
        for k, v in toks.items():
            if k == eng and eng == "pe":
                continue
            if self.seen.get((eng, k), 0) >= v:
                continue
            self.seen[(eng, k)] = v
            pend.append((k, v))
        for k, v in pend:
            self.E[eng].wait_ge(self.sems[k], v)

    def _record(self, tok, reads, writes):
        k, v = tok
        for b in reads:
            if b.r.get(k, 0) < v:
                b.r[k] = v
        for b in writes:
            b.w = tok
            b.r = {}

    def psum_banks(self):
        self.PP = [self.psum(f"pp{k}", [128, 1024], F32) for k in range(4)]
        self.BK = self.bufs(8, "bank")
        return self.PP, self.BK

    def bank(self, b):
        return self.PP[b // 2][:, (b % 2) * 512:(b % 2 + 1) * 512]

    def op(self, eng, fn, reads=(), writes=(), px=()):
        writes = list(writes) + list(px)
        toks = self._need(reads, writes)
        self._do_waits(eng, toks)
        ins = fn(self.E[eng])
        self.cnt[eng] += 1
        ins.then_inc(self.sems[eng], 1)
        tok = (eng, self.cnt[eng])
        self._record(tok, reads, writes)
        return tok

    def dma(self, eng, out, in_, reads=(), writes=(), sembuf=None, **kw):
        sb = sembuf or (writes[0] if writes else reads[0])
        if sb.sem is None:
            sb.sem = self.es.enter_context(self.nc.semaphore("dsem_" + sb.name))
            self.sems[("d", sb.name)] = sb.sem
        toks = self._need(reads, writes)
        if sb.semv:
            k = ("d", sb.name)
            if toks.get(k, 0) < sb.semv:
                toks[k] = sb.semv
        self._do_waits(eng, toks)
        ins = self.E[eng].dma_start(out=out, in_=in_, **kw)
        sb.semv += 16
        ins.then_inc(sb.sem, 16)
        tok = (("d", sb.name), sb.semv)
        self._record(tok, reads, writes)
        return tok

    def barrier(self):
        allk = {e: self.cnt[e] for e in self.ENG if self.cnt[e]}
        for e in self.ENG:
            self._do_waits(e, dict(allk))

    def wait_all_dma(self, eng, bufs):
        toks = {}
        for b in bufs:
            if b.sem is not None and b.semv:
                toks[("d", b.name)] = b.semv
        self._do_waits(eng, toks)


class SBState:
    pass


def sb_setup(P, nslots_kv=3):
    S = SBState()
    nc = P.nc
    S.Z = [P.bank(0), P.bank(1)]
    S.C = [P.bank(2), P.bank(3)]
    S.OT = [P.bank(4), P.bank(5), P.bank(6)]
    S.Zb = P.BK[0:2]; S.Cb = P.BK[2:4]; S.OTb = P.BK[4:7]
    S.e = [P.sbuf(f"sb_e{i}", [128, 512], F32) for i in range(3)]
    S.eb = P.bufs(3, "eb")
    S.sp = [P.sbuf(f"sb_sp{i}", [128, 512], BF16) for i in range(2)]
    S.spb = P.bufs(2, "spb")
    S.eg = [P.sbuf(f"sb_eg{i}", [128, 512], F32) for i in range(2)]
    S.egb = P.bufs(2, "egb")
    S.w = [P.sbuf(f"sb_w{i}", [128, 512], BF16) for i in range(2)]
    S.wb = P.bufs(2, "wb")
    S.ec = [P.sbuf(f"sb_ec{i}", [128, 4], F32) for i in range(2)]
    S.ecb = P.bufs(2, "ecb")
    S.carry = P.sbuf("sb_carry", [128, 4, 4], F32)
    S.carryb = P.bufs(4, "carryb")
    S.kT = [P.sbuf(f"sb_kT{i}", [128, 2, 1024], BF16) for i in range(nslots_kv)]
    S.v = [P.sbuf(f"sb_v{i}", [128, 8, 256], BF16) for i in range(nslots_kv)]
    S.kTb = P.bufs(nslots_kv, "kTb"); S.vb = P.bufs(nslots_kv, "vb")
    S.nkv = nslots_kv
    S.kvcount = 0
    return S


def sb_tile(P, S, C, m, qTa, qTab, ytile, yb, kT_all, v_all, kvb_dram, qcol0=None):
    nc = P.nc
    P.op("dve", lambda e: e.memset(S.carry[:], 0.0), writes=S.carryb)
    P.op("pool", lambda e: e.memset(ytile[:, :, 0:256], 0.0), writes=[b for hb in yb for b in hb])

    steps = []
    for G in range(4 * m + 3, -1, -1):
        for g in range(7, -1, -1):
            for h in range(4):
                steps.append((G, g, h))
    loaded = {}

    def load_group(G):
        slot = S.kvcount % S.nkv
        S.kvcount += 1
        P.dma("sp", S.kT[slot][:], kT_all[:, 1024 * G:1024 * (G + 1)].rearrange("(two p) n -> p two n", p=128),
              reads=[kvb_dram], writes=[S.kTb[slot]])
        P.dma("sp", S.v[slot][:], v_all[1024 * G:1024 * (G + 1), :].rearrange("(blk p) f -> p blk f", p=128),
              reads=[kvb_dram], writes=[S.vb[slot]])
        loaded[G] = slot

    n = len(steps)
    info = [None] * n

    def geom(i):
        G, g, h = steps[i]
        jmin = max(0, G - 4 * m)
        N = (4 - jmin) * 128
        masked = G >= 4 * m
        return G, g, h, jmin, N, masked

    def S0(i):
        G, g, h, jmin, N, masked = geom(i)
        if G not in loaded:
            load_group(G)
        if G - 1 >= 0 and (G - 1) not in loaded and S.nkv >= 3:
            load_group(G - 1)
        slot = loaded[G]
        pb = (h % 2) * 64
        z = i % 2
        q0 = ((4 * m) * 128 if qcol0 is None else qcol0) + jmin * 128
        P.op("pe", lambda e: e.matmul(S.Z[z][:, 0:N], lhsT=S.kT[slot][pb:pb + 64, h // 2, g * 128:(g + 1) * 128],
                                      rhs=qTa[pb:pb + 64, h // 2, q0:q0 + N], start=True, stop=True),
             reads=[S.kTb[slot], qTab], px=[S.Zb[z]])

    def S1a(i):
        G, g, h, jmin, N, masked = geom(i)
        z = i % 2; e3 = i % 3
        P.op("act", lambda e: e.activation(out=S.e[e3][:, 0:N], in_=S.Z[z][:, 0:N], func=AF.Exp),
             writes=[S.eb[e3]], px=[S.Zb[z]])
        if masked:
            P.op("pool", lambda e: e.tensor_tensor(out=S.e[e3][:, 0:128], in0=S.e[e3][:, 0:128],
                                                   in1=C.mask[:, g, :], op=ALU.mult),
                 reads=[C.cb], writes=[S.eb[e3]])

    def S1b(i):
        G, g, h, jmin, N, masked = geom(i)
        e3 = i % 3; s2 = i % 2; o3 = i % 3
        P.op("act", lambda e: e.activation(out=S.sp[s2][:, 0:N], in_=S.e[e3][:, 0:N], func=AF.Ln, bias=1.0),
             reads=[S.eb[e3]], writes=[S.spb[s2]])
        P.op("pe", lambda e: e.matmul(S.C[s2][:, 0:N], lhsT=C.Ap[:], rhs=S.sp[s2][:, 0:N], start=True, stop=True),
             reads=[S.spb[s2], C.cb], px=[S.Cb[s2]])
        for j in range(jmin, 4):
            c0 = (j - jmin) * 128
            P.op("pe", lambda e: e.matmul(S.OT[o3][:, 256 + j:257 + j], lhsT=S.sp[s2][:, c0:c0 + 128], rhs=C.ones[:, 0:1],
                                          start=True, stop=True),
                 reads=[S.spb[s2], C.cb], px=[S.OTb[o3]])

    def S2a(i):
        G, g, h, jmin, N, masked = geom(i)
        s2 = i % 2
        P.op("act", lambda e: e.activation(out=S.eg[s2][:, 0:N], in_=S.C[s2][:, 0:N], func=AF.Exp, scale=-1.0),
             writes=[S.egb[s2]], px=[S.Cb[s2]])
        P.op("act", lambda e: e.activation(out=S.ec[s2][:, jmin:4], in_=S.carry[:, h, jmin:4], func=AF.Exp, scale=-1.0),
             reads=[S.carryb[h]], writes=[S.ecb[s2]])

    def S2b(i):
        G, g, h, jmin, N, masked = geom(i)
        e3 = i % 3; s2 = i % 2; o3 = i % 3
        slot = loaded[G]
        P.op("pool", lambda e: e.tensor_tensor(out=S.w[s2][:, 0:N], in0=S.e[e3][:, 0:N], in1=S.eg[s2][:, 0:N],
                                               op=ALU.mult),
             reads=[S.eb[e3], S.egb[s2]], writes=[S.wb[s2]])
        for j in range(jmin, 4):
            c0 = (j - jmin) * 128
            P.op("pe", lambda e: e.matmul(S.OT[o3][:, j * 64:(j + 1) * 64], lhsT=S.w[s2][:, c0:c0 + 128],
                                          rhs=S.v[slot][:, g, h * 64:(h + 1) * 64], start=True, stop=True),
                 reads=[S.wb[s2], S.vb[slot]], px=[S.OTb[o3]])

    def S3(i):
        G, g, h, jmin, N, masked = geom(i)
        s2 = i % 2; o3 = i % 3
        for j in range(jmin, 4):
            P.op("dve", lambda e: e.scalar_tensor_tensor(out=ytile[:, j, h * 64:(h + 1) * 64],
                                                         in0=S.OT[o3][:, j * 64:(j + 1) * 64],
                                                         scalar=S.ec[s2][:, j:j + 1],
                                                         in1=ytile[:, j, h * 64:(h + 1) * 64],
                                                         op0=ALU.mult, op1=ALU.add),
                 reads=[S.ecb[s2]], writes=[yb[h][j]], px=[S.OTb[o3]])
        P.op("dve", lambda e: e.tensor_tensor(out=S.carry[:, h, jmin:4], in0=S.OT[o3][:, 256 + jmin:260],
                                              in1=S.carry[:, h, jmin:4], op=ALU.add),
             writes=[S.carryb[h]], px=[S.OTb[o3]])

    for i in range(n + 3):
        if i < n:
            S0(i)
        if 1 <= i <= n:
            S1a(i - 1)
        if 2 <= i <= n + 1:
            S2a(i - 2)
        if 1 <= i <= n:
            S1b(i - 1)
        if 2 <= i <= n + 1:
            S2b(i - 2)
        if i >= 3:
            S3(i - 3)


EPS = 1e-6
NT = 2048
NB = 16


def load_w_bf16(P, dst, dstb, src, col0, ncols, nchunk=8, maxcols=1536):
    c = 0
    while c < ncols:
        n = min(maxcols, ncols - c)
        P.dma("pool", dst[:, :, c:c + n],
              src[:, col0 + c:col0 + c + n].rearrange("(dc p) n -> p dc n", p=128),
              writes=[dstb])
        c += n


def build_a(nc, es, dbg=None):
    P = Prog(nc, es)
    D = lambda name, shape, dt, kind: nc.dram_tensor(name, shape, dt, kind=kind).ap()
    x_d = D("x", [NT, 1024], F32, "ExternalInput")
    anw_d = D("anwT", [128, 8], F32, "ExternalInput")
    w_d = D("w_in", [1024, 3072], F32, "ExternalInput")
    gq_d = D("gq", [128, 512], F32, "ExternalInput")
    gk_d = D("gk", [128, 512], F32, "ExternalInput")
    idf_d = D("identf", [128, 128], F32, "ExternalInput")
    idb_d = D("identb", [128, 128], BF16, "ExternalInput")
    o_qTa = D("o_qTa", [256, NT], BF16, "ExternalOutput")
    o_kTa = D("o_kTa", [256, NT], BF16, "ExternalOutput")
    o_va = D("o_va", [NT, 256], BF16, "ExternalOutput")
    o_qTb = D("o_qTb", [512, NT], BF16, "ExternalOutput")
    o_kTb = D("o_kTb", [512, NT], BF16, "ExternalOutput")
    o_vb = D("o_vb", [NT, 512], BF16, "ExternalOutput")
    o_gb = D("o_gb", [NT, 256], BF16, "ExternalOutput")
    o_hcT = D("o_hcT", [256, NT], F32, "ExternalOutput")

    w_sb = P.sbuf("w_sb", [128, 8, 3072], BF16); wb = P.buf("wb")
    anw = P.sbuf("anw", [128, 8], F32)
    gq = P.sbuf("gq_sb", [128, 512], F32); gk = P.sbuf("gk_sb", [128, 512], F32)
    idf = P.sbuf("idf", [128, 128], F32); idb = P.sbuf("idb", [128, 128], BF16)
    cb = P.buf("consts")
    cbs = []
    for dst, src in ((anw, anw_d), (gq, gq_d), (gk, gk_d), (idf, idf_d), (idb, idb_d)):
        cbs.append(P.buf("ld_" + str(P.nbuf)))
        P.dma("sp", dst[:], src[:, :], writes=[cbs[-1]])
    cdum = P.sbuf("cdum", [128, 8], F32)
    P.op("dve", lambda e: e.memset(cdum[:], 0.0), reads=cbs, writes=[cb])
    load_w_bf16(P, w_sb, wb, w_d, 0, 3072)

    xb = [P.sbuf(f"xb{i}", [128, 1024], F32) for i in range(2)]; xbb = P.bufs(2, "xbb")
    xs = [P.sbuf(f"xs{i}", [128, 1024], F32) for i in range(2)]; xsb = P.bufs(2, "xsb")
    junk = P.sbuf("junk", [128, 1024], F32); junkb = P.buf("junkb")
    ss = P.sbuf("ss", [128, 4], F32); ssb = P.buf("ssb")
    hT = [P.sbuf(f"hT{i}", [128, 8, 512], BF16) for i in range(2)]; hTb = P.bufs(2, "hTb")
    tp = [P.psum(f"tp{i}", [128, 4, 128], F32) for i in range(2)]; tpb = P.bufs(2, "tpb")
    pf = [P.psum(f"pf{i}", [128, 512], F32) for i in range(2)]; pfb = P.bufs(2, "pfb")
    pt = [P.psum(f"pt{i}", [128, 512], F32) for i in range(2)]; ptb = P.bufs(2, "ptb")
    ptr = P.psum("ptr", [128, 4, 128], BF16); ptrb = P.buf("ptrb")
    st = []
    for s in range(2):
        d = {}
        d["qTa"] = P.sbuf(f"st_qTa{s}", [128, 2, 512], BF16)
        d["kTa"] = P.sbuf(f"st_kTa{s}", [128, 2, 512], BF16)
        d["hcT"] = P.sbuf(f"st_hcT{s}", [128, 2, 512], F32)
        d["va"] = P.sbuf(f"st_va{s}", [128, 4, 256], BF16)
        d["vb"] = P.sbuf(f"st_vb{s}", [128, 4, 512], BF16)
        d["gb"] = P.sbuf(f"st_gb{s}", [128, 4, 256], BF16)
        d["qTb"] = P.sbuf(f"st_qTb{s}", [128, 4, 512], BF16)
        d["kTb"] = P.sbuf(f"st_kTb{s}", [128, 4, 512], BF16)
        d["b"] = {k: P.buf(f"st_{k}{s}") for k in ("qTa", "kTa", "hcT", "va", "vb", "gb", "qTb", "kTb")}
        st.append(d)
    gct = [P.sbuf(f"gct{i}", [128, 512], F32) for i in range(2)]; gctb = P.bufs(2, "gctb")
    tq = P.sbuf("tq", [128, 512], F32); tqb = P.buf("tqb")
    sqb_t = P.sbuf("sqb", [128, 512], F32); sqbb = P.buf("sqbb")
    ssq = P.sbuf("ssq", [128, 8], F32); ssqb = P.buf("ssqb")
    rq = P.sbuf("rq", [128, 8], F32); rqb = P.buf("rqb")
    t1 = P.sbuf("t1", [128, 512], F32); t1b = P.buf("t1b")
    qn = P.sbuf("qn", [128, 512], BF16); qnb = P.buf("qnb")

    pfc = [0]; ptc = [0]

    def rmsnorm_block_to_hT(xsrc_ap, xsrc_buf, i, hTt, hTtb, col0, wT, wTb):
        s = i % 2
        P.op("act", lambda e: e.activation(out=junk[:], in_=xsrc_ap, func=AF.Square, accum_out=ss[:, 0:1]),
             reads=[xsrc_buf], writes=[junkb, ssb])
        P.op("act", lambda e: e.activation(out=ss[:, 1:2], in_=ss[:, 0:1], func=AF.Ln, scale=1.0 / 1024, bias=EPS),
             writes=[ssb])
        P.op("act", lambda e: e.activation(out=ss[:, 2:3], in_=ss[:, 1:2], func=AF.Exp, scale=-0.5), writes=[ssb])
        P.op("dve", lambda e: e.tensor_scalar(out=xs[s][:], in0=xsrc_ap, scalar1=ss[:, 2:3], scalar2=None, op0=ALU.mult),
             reads=[xsrc_buf, ssb], writes=[xsb[s]])
        for half in range(2):
            for k in range(4):
                dc = half * 4 + k
                P.op("pe", lambda e: e.transpose(out=tp[half][:, k, :], in_=xs[s][:, dc * 128:(dc + 1) * 128], identity=idf[:]),
                     reads=[xsb[s], cb], writes=[tpb[half]])
            for k in range(4):
                dc = half * 4 + k
                P.op("dve", lambda e: e.tensor_scalar(out=hTt[:, dc, col0:col0 + 128], in0=tp[half][:, k, :],
                                                      scalar1=wT[:, dc:dc + 1], scalar2=None, op0=ALU.mult),
                     reads=[tpb[half], wTb], writes=[hTtb])

    def acc8(out_ap, outb, lhs, rhs, reads):
        for dc in range(8):
            P.op("pe", lambda e: e.matmul(out_ap, lhsT=lhs(dc), rhs=rhs(dc), start=(dc == 0), stop=(dc == 7)),
                 reads=reads, writes=[outb])

    for tt in range(4):
        ts = tt % 2
        S = st[ts]
        for j in range(4):
            i = 4 * tt + j
            s = i % 2
            P.dma("sp", xb[s][:], x_d[i * 128:(i + 1) * 128, :], writes=[xbb[s]])
            rmsnorm_block_to_hT(xb[s][:], xbb[s], i, hT[ts], hTb[ts], j * 128, anw, cb)
        fm = [("qTa", 0, 0), ("qTa", 1, 128), ("kTa", 0, 256), ("kTa", 1, 384),
              ("gc", 0, 2560), ("gc", 1, 2688), ("xc", 0, 2816), ("xc", 1, 2944)]
        for name, cc, col0 in fm:
            z = pfc[0] % 2; pfc[0] += 1
            acc8(pf[z][:, :], pfb[z], lambda dc: w_sb[:, dc, col0:col0 + 128], lambda dc: hT[ts][:, dc, :],
                 [wb, hTb[ts]])
            if name == "qTa":
                P.op("act", lambda e: e.activation(out=S["qTa"][:, cc, :], in_=pf[z][:, :], func=AF.Copy, scale=0.125),
                     reads=[pfb[z]], writes=[S["b"]["qTa"]])
            elif name == "kTa":
                P.op("act", lambda e: e.activation(out=S["kTa"][:, cc, :], in_=pf[z][:, :], func=AF.Copy),
                     reads=[pfb[z]], writes=[S["b"]["kTa"]])
            elif name == "gc":
                P.op("act", lambda e: e.activation(out=gct[cc][:], in_=pf[z][:, :], func=AF.Copy),
                     reads=[pfb[z]], writes=[gctb[cc]])
            else:
                P.op("dve", lambda e: e.tensor_tensor(out=S["hcT"][:, cc, :], in0=pf[z][:, :], in1=gct[cc][:], op=ALU.mult),
                     reads=[pfb[z], gctb[cc]], writes=[S["b"]["hcT"]])
        for j in range(4):
            blk = lambda dc: hT[ts][:, dc, j * 128:(j + 1) * 128]
            z = ptc[0] % 2; ptc[0] += 1
            acc8(pt[z][:, 0:256], ptb[z], blk, lambda dc: w_sb[:, dc, 512:768], [wb, hTb[ts]])
            acc8(pt[z][:, 256:512], ptb[z], blk, lambda dc: w_sb[:, dc, 2304:2560], [wb, hTb[ts]])
            P.op("act", lambda e: e.activation(out=S["va"][:, j, :], in_=pt[z][:, 0:256], func=AF.Copy),
                 reads=[ptb[z]], writes=[S["b"]["va"]])
            P.op("act", lambda e: e.activation(out=S["gb"][:, j, :], in_=pt[z][:, 256:512], func=AF.Copy),
                 reads=[ptb[z]], writes=[S["b"]["gb"]])
            z = ptc[0] % 2; ptc[0] += 1
            acc8(pt[z][:, :], ptb[z], blk, lambda dc: w_sb[:, dc, 1792:2304], [wb, hTb[ts]])
            P.op("act", lambda e: e.activation(out=S["vb"][:, j, :], in_=pt[z][:, :], func=AF.Copy),
                 reads=[ptb[z]], writes=[S["b"]["vb"]])
            for which, col0, gain, lnb, dstname in (("q", 768, gq, math.log(0.125), "qTb"), ("k", 1280, gk, 0.0, "kTb")):
                z = ptc[0] % 2; ptc[0] += 1
                acc8(pt[z][:, :], ptb[z], blk, lambda dc: w_sb[:, dc, col0:col0 + 512], [wb, hTb[ts]])
                P.op("act", lambda e: e.activation(out=tq[:], in_=pt[z][:, :], func=AF.Copy), reads=[ptb[z]], writes=[tqb])
                P.op("pool", lambda e: e.tensor_tensor(out=sqb_t[:], in0=tq[:], in1=tq[:], op=ALU.mult),
                     reads=[tqb], writes=[sqbb])
                P.op("dve", lambda e: e.tensor_reduce(out=ssq[:], in_=sqb_t[:].rearrange("p (h d) -> p h d", h=8),
                                                      axis=AX.X, op=ALU.add), reads=[sqbb], writes=[ssqb])
                P.op("act", lambda e: e.activation(out=rq[:], in_=ssq[:], func=AF.Ln, scale=1.0 / 64, bias=EPS),
                     reads=[ssqb], writes=[rqb])
                P.op("act", lambda e: e.activation(out=rq[:], in_=rq[:], func=AF.Exp, scale=-0.5, bias=lnb), writes=[rqb])
                P.op("dve", lambda e: e.tensor_tensor(out=t1[:].rearrange("p (h d) -> p h d", h=8),
                                                      in0=tq[:].rearrange("p (h d) -> p h d", h=8),
                                                      in1=rq[:].unsqueeze(2).broadcast_to([128, 8, 64]), op=ALU.mult),
                     reads=[tqb, rqb], writes=[t1b])
                P.op("pool", lambda e: e.tensor_tensor(out=qn[:], in0=t1[:], in1=gain[:], op=ALU.mult),
                     reads=[t1b, cb], writes=[qnb])
                for pr in range(4):
                    P.op("pe", lambda e: e.transpose(out=ptr[:, pr, :], in_=qn[:, pr * 128:(pr + 1) * 128], identity=idb[:]),
                         reads=[qnb, cb], writes=[ptrb])
                P.op("act", lambda e: e.activation(out=S[dstname][:, :, j * 128:(j + 1) * 128], in_=ptr[:], func=AF.Copy),
                     reads=[ptrb], writes=[S["b"][dstname]])
        t0 = tt * 512
        P.dma("sp", o_qTa[:, t0:t0 + 512].rearrange("(two p) n -> p two n", p=128), S["qTa"][:], reads=[S["b"]["qTa"]])
        P.dma("sp", o_kTa[:, t0:t0 + 512].rearrange("(two p) n -> p two n", p=128), S["kTa"][:], reads=[S["b"]["kTa"]])
        P.dma("sp", o_hcT[:, t0:t0 + 512].rearrange("(two p) n -> p two n", p=128), S["hcT"][:], reads=[S["b"]["hcT"]])
        P.dma("sp", o_qTb[:, t0:t0 + 512].rearrange("(f p) n -> p f n", p=128), S["qTb"][:], reads=[S["b"]["qTb"]])
        P.dma("sp", o_kTb[:, t0:t0 + 512].rearrange("(f p) n -> p f n", p=128), S["kTb"][:], reads=[S["b"]["kTb"]])
        P.dma("sp", o_va[t0:t0 + 512, :].rearrange("(j p) f -> p j f", p=128), S["va"][:], reads=[S["b"]["va"]])
        P.dma("sp", o_vb[t0:t0 + 512, :].rearrange("(j p) f -> p j f", p=128), S["vb"][:], reads=[S["b"]["vb"]])
        P.dma("sp", o_gb[t0:t0 + 512, :].rearrange("(j p) f -> p j f", p=128), S["gb"][:], reads=[S["b"]["gb"]])
    allb = []
    for s in range(2):
        allb += list(st[s]["b"].values())
    P.wait_all_dma("sp", allb)
    print("A instr counts", P.cnt)
    return P


S_TOT = 16384


class Consts:
    pass


def acc_n(P, out_ap, outb, n, lhs, rhs, reads):
    for dc in range(n):
        P.op("pe", lambda e: e.matmul(out_ap, lhsT=lhs(dc), rhs=rhs(dc), start=(dc == 0), stop=(dc == n - 1)),
             reads=reads, px=[outb])


def build_b1(nc, es, ntiles=4):
    P = Prog(nc, es)
    D = lambda name, shape, dt, kind: nc.dram_tensor(name, shape, dt, kind=kind).ap()
    x_d = D("x", [NT, 1024], F32, "ExternalInput")
    qTa_d = D("qTa", [256, NT], BF16, "ExternalInput")
    kTa_d = D("kTa_all", [256, S_TOT], BF16, "ExternalInput")
    va_d = D("va_all", [S_TOT, 256], BF16, "ExternalInput")
    qTb_d = D("qTb", [512, NT], BF16, "ExternalInput")
    kTbL_d = D("kTbL", [NB, 512, 640], BF16, "ExternalInput")
    vbL_d = D("vbL", [NB, 640, 520], BF16, "ExternalInput")
    bias_d = D("biasT", [128, 8, 640], BF16, "ExternalInput")
    hcL_d = D("hcL", [256, NB, 130], F32, "ExternalInput")
    cw_d = D("cw", [128, 6], F32, "ExternalInput")
    gb_d = D("gb", [NT, 256], BF16, "ExternalInput")
    mask_d = D("mask", [128, 1024], BF16, "ExternalInput")
    Ap_d = D("Ap", [128, 128], BF16, "ExternalInput")
    idf_d = D("identf", [128, 128], F32, "ExternalInput")
    onw_d = D("onw", [128, 1024], F32, "ExternalInput")
    wout_d = D("w_out", [1024, 1024], F32, "ExternalInput")
    xo_d = D("xo", [NT, 1024], F32, "ExternalOutput")

    C = Consts()
    C.Ap = P.sbuf("c_Ap", [128, 128], BF16)
    C.ones = P.sbuf("c_ones", [128, 8], BF16)
    C.mask = P.sbuf("c_mask", [128, 8, 128], BF16)
    C.cb = P.buf("cb")
    idf = P.sbuf("idf", [128, 128], F32)
    bias = P.sbuf("bias", [128, 8, 640], BF16)
    cw = P.sbuf("cw_sb", [128, 6], F32)
    onw = P.sbuf("onw_sb", [128, 1024], F32)
    wout = P.sbuf("wout", [128, 8, 1024], BF16); woutb = P.buf("woutb")
    cbs = []
    for dst, src in ((C.Ap[:], Ap_d[:, :]), (C.mask[:], mask_d.rearrange("p (g q) -> p g q", g=8)), (idf[:], idf_d[:, :]),
                     (bias[:], bias_d[:, :, :]), (cw[:], cw_d[:, :]), (onw[:], onw_d[:, :])):
        cbs.append(P.buf("ld_" + str(P.nbuf)))
        P.dma("sp", dst, src, writes=[cbs[-1]])
    P.op("dve", lambda e: e.memset(C.ones[:], 1.0), reads=cbs, writes=[C.cb])
    P.psum_banks()
    load_w_bf16(P, wout, woutb, wout_d, 0, 1024, maxcols=1024)

    kvd = P.buf("kvd")
    qTa = P.sbuf("qTa_sb", [128, 2, 512], BF16); qTab = P.buf("qTab")
    qTb = P.sbuf("qTb_sb", [128, 4, 512], BF16); qTbb = P.buf("qTbb")
    ytile = P.sbuf("ytile", [128, 4, 1024], F32)
    yb = [P.bufs(4, f"yb{h}_") for h in range(4)]
    ycb = P.bufs(4, "ycb")
    yvb = P.bufs(4, "yvb")
    kL = [P.sbuf(f"kL{i}", [128, 4, 640], BF16) for i in range(2)]; kLb = P.bufs(2, "kLb")
    vL = [P.sbuf(f"vL{i}", [128, 5, 520], BF16) for i in range(2)]; vLb = P.bufs(2, "vLb")
    ps_s = [P.PP[0]]
    bk01 = [P.BK[0], P.BK[1]]
    tsb = [P.sbuf(f"tsb{i}", [128, 640], F32) for i in range(2)]; tsbb = P.bufs(2, "tsbb")
    pb_ = [P.sbuf(f"pbf{i}", [128, 640], BF16) for i in range(2)]; pbb = P.bufs(2, "pbb")
    po_ap = [P.bank(2), P.bank(3)]; pob = [P.BK[2], P.BK[3]]
    pcv = P.bank(7); pcvb = P.BK[7]
    S = sb_setup(P, nslots_kv=3)
    rden = P.sbuf("rden", [128, 2], F32); rdenb = P.bufs(2, "rdenb")
    hc = [P.sbuf(f"hc{i}", [128, 2, 130], F32) for i in range(2)]; hcb = P.bufs(2, "hcb")
    gbs = [P.sbuf(f"gbs{i}", [128, 256], BF16) for i in range(2)]; gbsb = P.bufs(2, "gbsb")
    cacc = P.sbuf("cacc", [128, 128], F32); caccb = P.buf("caccb")
    g1 = P.sbuf("g1", [128, 1024], F32); g1b = P.buf("g1b")
    g2 = P.sbuf("g2", [128, 1024], F32); g2b = P.buf("g2b")
    ss16 = P.sbuf("ss16", [128, 16], F32); ss16b = P.buf("ss16b")
    ynT = P.sbuf("ynT", [128, 8, 512], BF16); ynTb = P.buf("ynTb")
    xb = [P.sbuf(f"xb{i}", [128, 1024], F32) for i in range(2)]; xbb = P.bufs(2, "xbb")

    def chunk_attn(m, j):
        i = 4 * m + j
        s = i % 2
        P.dma("sp", kL[s][:], kTbL_d[i].rearrange("(f p) n -> p f n", p=128), writes=[kLb[s]])
        P.dma("sp", vL[s][:], vbL_d[i].rearrange("(r p) n -> p r n", p=128), writes=[vLb[s]])
        for h in range(8):
            pbase = (h % 2) * 64
            u = h % 2
            for r in range(5):
                P.op("pe", lambda e: e.matmul(ps_s[0][:, r * 128:(r + 1) * 128],
                                              lhsT=kL[s][pbase:pbase + 64, h // 2, r * 128:(r + 1) * 128],
                                              rhs=qTb[pbase:pbase + 64, h // 2, j * 128:(j + 1) * 128], start=True, stop=True),
                     reads=[kLb[s], qTbb], px=bk01)
            P.op("dve", lambda e: e.tensor_tensor(out=tsb[u][:], in0=ps_s[0][:, 0:640], in1=bias[:, h, :], op=ALU.add),
                 reads=[C.cb], writes=[tsbb[u]], px=bk01)
            P.op("act", lambda e: e.activation(out=pb_[u][:], in_=tsb[u][:], func=AF.Exp), reads=[tsbb[u]], writes=[pbb[u]])
            for r in range(5):
                P.op("pe", lambda e: e.matmul(po_ap[u][:, 0:65], lhsT=pb_[u][:, r * 128:(r + 1) * 128],
                                              rhs=vL[s][:, r, h * 65:(h + 1) * 65], start=(r == 0), stop=(r == 4)),
                     reads=[pbb[u], vLb[s]], px=[pob[u]])
            P.op("dve", lambda e: e.reciprocal(out=rden[:, u:u + 1], in_=po_ap[u][:, 64:65]),
                 writes=[rdenb[u]], px=[pob[u]])
            P.op("dve", lambda e: e.tensor_scalar(out=ytile[:, j, 256 + h * 64:256 + (h + 1) * 64],
                                                  in0=po_ap[u][:, 0:64], scalar1=rden[:, u:u + 1],
                                                  scalar2=None, op0=ALU.mult),
                 reads=[rdenb[u]], writes=[ycb[j]], px=[pob[u]])

    def conv(m, j):
        i = 4 * m + j
        s = i % 2
        P.dma("sp", hc[s][:], hcL_d[:, i, :].rearrange("(cc p) t -> p cc t", p=128), writes=[hcb[s]])
        P.dma("sp", gbs[s][:], gb_d[i * 128:(i + 1) * 128, :], writes=[gbsb[s]])
        for cc in range(2):
            P.op("dve", lambda e: e.tensor_scalar(out=cacc[:], in0=hc[s][:, cc, 2:130], scalar1=cw[:, cc * 3 + 2:cc * 3 + 3],
                                                  scalar2=None, op0=ALU.mult), reads=[hcb[s], C.cb], writes=[caccb])
            for k in (1, 0):
                P.op("dve", lambda e: e.scalar_tensor_tensor(out=cacc[:], in0=hc[s][:, cc, k:k + 128],
                                                             scalar=cw[:, cc * 3 + k:cc * 3 + k + 1], in1=cacc[:],
                                                             op0=ALU.mult, op1=ALU.add), reads=[hcb[s], C.cb], writes=[caccb])
            P.op("pe", lambda e: e.transpose(out=pcv[:, 0:128], in_=cacc[:], identity=idf[:]),
                 reads=[caccb, C.cb], px=[pcvb])
            P.op("dve", lambda e: e.tensor_tensor(out=ytile[:, j, 768 + cc * 128:768 + (cc + 1) * 128], in0=pcv[:, 0:128],
                                                  in1=gbs[s][:, cc * 128:(cc + 1) * 128], op=ALU.mult),
                 reads=[gbsb[s]], writes=[yvb[j]], px=[pcvb])

    def gn_outproj(m, j):
        i = 4 * m + j
        s = i % 2
        yall = [b for hb in yb for b in [hb[j]]] + [ycb[j], yvb[j]]
        y = ytile[:, j, :]
        P.op("pool", lambda e: e.tensor_tensor(out=g1[:], in0=y, in1=y, op=ALU.mult), reads=yall, writes=[g1b])
        P.op("dve", lambda e: e.tensor_reduce(out=ss16[:], in_=g1[:].rearrange("p (g d) -> p g d", g=16), axis=AX.X, op=ALU.add),
             reads=[g1b], writes=[ss16b])
        P.op("act", lambda e: e.activation(out=ss16[:], in_=ss16[:], func=AF.Ln, scale=1.0 / 64, bias=EPS), writes=[ss16b])
        P.op("act", lambda e: e.activation(out=ss16[:], in_=ss16[:], func=AF.Exp, scale=-0.5), writes=[ss16b])
        P.op("dve", lambda e: e.tensor_tensor(out=g2[:].rearrange("p (g d) -> p g d", g=16),
                                              in0=ytile[:, j, :].rearrange("p (g d) -> p g d", g=16),
                                              in1=ss16[:].unsqueeze(2).broadcast_to([128, 16, 64]), op=ALU.mult),
             reads=yall + [ss16b], writes=[g2b])
        P.op("pool", lambda e: e.tensor_tensor(out=g2[:], in0=g2[:], in1=onw[:], op=ALU.mult), reads=[C.cb], writes=[g2b])
        for half in range(2):
            for k in range(4):
                dc = half * 4 + k
                P.op("pe", lambda e: e.transpose(out=P.PP[2][:, half * 512 + k * 128:half * 512 + (k + 1) * 128],
                                                 in_=g2[:, dc * 128:(dc + 1) * 128], identity=idf[:]),
                     reads=[g2b, C.cb], px=[P.BK[4 + half]])
        for half in range(2):
            P.op("act", lambda e: e.activation(out=ynT[:, half * 4:(half + 1) * 4, j * 128:(j + 1) * 128],
                                               in_=P.bank(4 + half).rearrange("p (c t) -> p c t", c=4), func=AF.Copy),
                 writes=[ynTb], px=[P.BK[4 + half]])
        P.dma("sp", xb[s][:], x_d[i * 128:(i + 1) * 128, :], writes=[xbb[s]])
        for half in range(2):
            acc_n(P, P.bank(6 + half), P.BK[6 + half], 8,
                  lambda dc: ynT[:, dc, j * 128:(j + 1) * 128], lambda dc: wout[:, dc, half * 512:(half + 1) * 512],
                  [ynTb, woutb])
            P.op("dve", lambda e: e.tensor_tensor(out=xb[s][:, half * 512:(half + 1) * 512], in0=P.bank(6 + half),
                                                  in1=xb[s][:, half * 512:(half + 1) * 512], op=ALU.add),
                 writes=[xbb[s]], px=[P.BK[6 + half]])
        P.dma("sp", xo_d[i * 128:(i + 1) * 128, :], xb[s][:], reads=[xbb[s]])

    import os
    ntiles = int(os.environ.get("B1_NT", ntiles))
    skip = set(os.environ.get("B1_SKIP", "").split(","))
    for m in range(ntiles):
        P.dma("sp", qTa[:], qTa_d[:, m * 512:(m + 1) * 512].rearrange("(two p) n -> p two n", p=128), writes=[qTab])
        P.dma("sp", qTb[:], qTb_d[:, m * 512:(m + 1) * 512].rearrange("(f p) n -> p f n", p=128), writes=[qTbb])
        if "sb" in skip:
            P.op("pool", lambda e: e.memset(ytile[:, :, 0:256], 0.5), writes=[b for hb in yb for b in hb])
        else:
            sb_tile(P, S, C, m, qTa, qTab, ytile, yb, kTa_d, va_d, kvd, qcol0=0)
        for j in range(4):
            if "chunk" in skip:
                P.op("pool", lambda e: e.memset(ytile[:, j, 256:768], 0.5), writes=[ycb[j]])
            else:
                chunk_attn(m, j)
            if "conv" in skip:
                P.op("pool", lambda e: e.memset(ytile[:, j, 768:1024], 0.5), writes=[yvb[j]])
            else:
                conv(m, j)
        for j in range(4):
            if "gn" not in skip:
                gn_outproj(m, j)
    P.wait_all_dma("sp", xbb)
    print("B1 instr counts", P.cnt)
    return P


FGROUPS = [(0, 6), (6, 6), (12, 5), (17, 5)]


def build_b2(nc, es):
    P = Prog(nc, es)
    D = lambda name, shape, dt, kind: nc.dram_tensor(name, shape, dt, kind=kind).ap()
    x_d = D("x", [NT, 1024], F32, "ExternalInput")
    fnw_d = D("fnwT", [128, 8], F32, "ExternalInput")
    idf_d = D("identf", [128, 128], F32, "ExternalInput")
    wg_d = D("w_gate", [1024, 2816], F32, "ExternalInput")
    wu_d = D("w_up", [1024, 2816], F32, "ExternalInput")
    wd_d = D("w_down", [2816, 1024], F32, "ExternalInput")
    xo_d = D("xo", [NT, 1024], F32, "ExternalOutput")

    cb = P.buf("cb")
    idf = P.sbuf("idf", [128, 128], F32); fnw = P.sbuf("fnw", [128, 8], F32)
    l1 = P.buf("ld1"); l2 = P.buf("ld2")
    P.dma("sp", idf[:], idf_d[:, :], writes=[l1])
    P.dma("sp", fnw[:], fnw_d[:, :], writes=[l2])
    cdum = P.sbuf("cdum", [128, 8], F32)
    P.op("dve", lambda e: e.memset(cdum[:], 0.0), reads=[l1, l2], writes=[cb])
    xres = P.sbuf("xres", [128, NB, 1024], F32); xrb = P.bufs(NB, "xrb")
    hnT = P.sbuf("hnT", [128, 8, NT], BF16); hnTb = P.buf("hnTb")
    aT = P.sbuf("aT", [128, 6, NT], BF16); aTb = P.bufs(6, "aTb")
    wg = P.sbuf("wg", [128, 8, 768], BF16); wgb = P.buf("wgb")
    wu = P.sbuf("wu", [128, 8, 768], BF16); wub = P.buf("wub")
    wd = P.sbuf("wd", [128, 6, 1024], BF16); wdb = P.buf("wdb")
    xs = [P.sbuf(f"xs{i}", [128, 1024], F32) for i in range(2)]; xsb = P.bufs(2, "xsb")
    junk = P.sbuf("junk", [128, 1024], F32); junkb = P.buf("junkb")
    ss = P.sbuf("ss", [128, 4], F32); ssb = P.buf("ssb")
    sg = [P.sbuf(f"sg{i}", [128, 512], F32) for i in range(2)]; sgb = P.bufs(2, "sgb")
    tp = [P.psum(f"tp{i}", [128, 4, 128], F32) for i in range(2)]; tpb = P.bufs(2, "tpb")
    pg = [P.psum(f"pg{i}", [128, 512], F32) for i in range(2)]; pgb = P.bufs(2, "pgb")
    pu = [P.psum(f"pu{i}", [128, 512], F32) for i in range(2)]; pub = P.bufs(2, "pub")
    po = [P.psum(f"po{i}", [128, 512], F32) for i in range(2)]; pob = P.bufs(2, "pob")

    for i in range(NB):
        P.dma("sp", xres[:, i, :], x_d[i * 128:(i + 1) * 128, :], writes=[xrb[i]])
    for i in range(NB):
        s = i % 2
        xa = xres[:, i, :]
        P.op("act", lambda e: e.activation(out=junk[:], in_=xa, func=AF.Square, accum_out=ss[:, 0:1]),
             reads=[xrb[i]], writes=[junkb, ssb])
        P.op("act", lambda e: e.activation(out=ss[:, 1:2], in_=ss[:, 0:1], func=AF.Ln, scale=1.0 / 1024, bias=EPS), writes=[ssb])
        P.op("act", lambda e: e.activation(out=ss[:, 2:3], in_=ss[:, 1:2], func=AF.Exp, scale=-0.5), writes=[ssb])
        P.op("dve", lambda e: e.tensor_scalar(out=xs[s][:], in0=xa, scalar1=ss[:, 2:3], scalar2=None, op0=ALU.mult),
             reads=[xrb[i], ssb], writes=[xsb[s]])
        for half in range(2):
            for k in range(4):
                dc = half * 4 + k
                P.op("pe", lambda e: e.transpose(out=tp[half][:, k, :], in_=xs[s][:, dc * 128:(dc + 1) * 128], identity=idf[:]),
                     reads=[xsb[s], cb], writes=[tpb[half]])
            for k in range(4):
                dc = half * 4 + k
                P.op("dve", lambda e: e.tensor_scalar(out=hnT[:, dc, i * 128:(i + 1) * 128], in0=tp[half][:, k, :],
                                                      scalar1=fnw[:, dc:dc + 1], scalar2=None, op0=ALU.mult),
                     reads=[tpb[half], cb], writes=[hnTb])
    cnt = 0
    oc = 0
    for f0, nf in FGROUPS:
        c0 = f0 * 128
        load_w_bf16(P, wg[:, :, 0:nf * 128], wgb, wg_d, c0, nf * 128, maxcols=1024)
        load_w_bf16(P, wu[:, :, 0:nf * 128], wub, wu_d, c0, nf * 128, maxcols=1024)
        P.dma("pool", wd[:, 0:nf, :], wd_d[c0:c0 + nf * 128, :].rearrange("(f p) n -> p f n", p=128), writes=[wdb])
        for fl in range(nf):
            for tt in range(4):
                z = cnt % 2; cnt += 1
                acc_n(P, pg[z][:, :], pgb[z], 8, lambda dc: wg[:, dc, fl * 128:(fl + 1) * 128],
                      lambda dc: hnT[:, dc, tt * 512:(tt + 1) * 512], [wgb, hnTb])
                acc_n(P, pu[z][:, :], pub[z], 8, lambda dc: wu[:, dc, fl * 128:(fl + 1) * 128],
                      lambda dc: hnT[:, dc, tt * 512:(tt + 1) * 512], [wub, hnTb])
                P.op("act", lambda e: e.activation(out=sg[z][:], in_=pg[z][:, :], func=AF.Silu), reads=[pgb[z]], writes=[sgb[z]])
                P.op("dve", lambda e: e.tensor_tensor(out=aT[:, fl, tt * 512:(tt + 1) * 512], in0=pu[z][:, :], in1=sg[z][:],
                                                      op=ALU.mult), reads=[pub[z], sgb[z]], writes=[aTb[fl]])
        for i in range(NB):
            for half in range(2):
                z = oc % 2; oc += 1
                acc_n(P, po[z][:, :], pob[z], nf, lambda fl: aT[:, fl, i * 128:(i + 1) * 128],
                      lambda fl: wd[:, fl, half * 512:(half + 1) * 512], aTb[0:nf] + [wdb])
                P.op("dve", lambda e: e.tensor_tensor(out=xres[:, i, half * 512:(half + 1) * 512], in0=po[z][:, :],
                                                      in1=xres[:, i, half * 512:(half + 1) * 512], op=ALU.add),
                     reads=[pob[z]], writes=[xrb[i]])
    ob = P.bufs(NB, "ob")
    for i in range(NB):
        P.dma("sp", xo_d[i * 128:(i + 1) * 128, :], xres[:, i, :], reads=[xrb[i]], sembuf=ob[i])
    P.wait_all_dma("sp", ob)
    print("B2 instr counts", P.cnt)
    return P


bf = ml_dtypes.bfloat16
NCORES = 8


def local_tokens(c):
    return np.concatenate([np.arange(128) + (8 * i + c) * 128 for i in range(NB)])


_PROGS = {}


def get_prog(kind):
    if kind not in _PROGS:
        nc = bass.Bass("TRN2", target_bir_lowering=False)
        with ExitStack() as es:
            {"a": build_a, "b1": build_b1, "b2": build_b2}[kind](nc, es)
        _PROGS[kind] = nc
    return _PROGS[kind]


def to_global_cols(per_core, rows):
    out = np.zeros((rows, NB, NCORES, 128), per_core[0].dtype)
    for c in range(NCORES):
        out[:, :, c, :] = np.asarray(per_core[c]).reshape(rows, NB, 128)
    return out.reshape(rows, S_TOT)


def to_global_rows(per_core, cols):
    out = np.zeros((NB, NCORES, 128, cols), per_core[0].dtype)
    for c in range(NCORES):
        out[:, c] = np.asarray(per_core[c]).reshape(NB, 128, cols)
    return out.reshape(S_TOT, cols)


def bias_table(rel_bias_l):
    k_idx = (np.arange(5)[None, :, None] * 128 + np.arange(128)[:, None, None])
    q_idx = 512 + np.arange(128)[None, None, :]
    dist = q_idx - k_idx
    rel = np.clip(dist, -128, 128) + 128
    cq = (np.arange(128) // 64)[None, None, :]
    ck = k_idx // 64
    valid = (ck >= cq) & (ck <= cq + 8)
    out = np.zeros((128, 8, 5, 128), np.float32)
    for h in range(8):
        out[:, h] = np.where(valid, rel_bias_l[h][rel], np.float32(-30000.0))
    return out.reshape(128, 8, 640).astype(bf)


def sb_mask(c):
    tri = (np.arange(128)[:, None] < np.arange(128)[None, :]).astype(np.float32)
    m = np.zeros((128, 8, 128), np.float32)
    for g in range(8):
        if g < c:
            m[:, g, :] = 1.0
        elif g == c:
            m[:, g, :] = tri
    return m.reshape(128, 1024).astype(bf)


def kernel(x, attn_norm_w, w_in, q_norm_w, k_norm_w, rel_bias, conv_w, out_norm_w,
           w_out, ffn_norm_w, w_gate, w_up, w_down, _nlayers=4, _debug=None):
    f32 = np.float32
    x = np.asarray(x, f32)[0]
    P = {k: np.asarray(v, f32) for k, v in dict(attn_norm_w=attn_norm_w, w_in=w_in, q_norm_w=q_norm_w,
         k_norm_w=k_norm_w, rel_bias=rel_bias, conv_w=conv_w, out_norm_w=out_norm_w, w_out=w_out,
         ffn_norm_w=ffn_norm_w, w_gate=w_gate, w_up=w_up, w_down=w_down).items()}
    toks = [local_tokens(c) for c in range(NCORES)]
    xs = [np.ascontiguousarray(x[toks[c]]) for c in range(NCORES)]
    identf = np.eye(128, dtype=f32)
    identb = identf.astype(bf)
    Ap = (np.arange(128)[:, None] >= np.arange(128)[None, :]).astype(bf)
    masks = [sb_mask(c) for c in range(NCORES)]
    cores = list(range(NCORES))
    for l in range(_nlayers):
        common = {
            "anwT": np.ascontiguousarray(P["attn_norm_w"][l].reshape(8, 128).T),
            "w_in": np.ascontiguousarray(P["w_in"][l]),
            "gq": np.ascontiguousarray(np.broadcast_to(np.tile(P["q_norm_w"][l], 8)[None, :], (128, 512))),
            "gk": np.ascontiguousarray(np.broadcast_to(np.tile(P["k_norm_w"][l], 8)[None, :], (128, 512))),
            "identf": identf, "identb": identb,
        }
        ra = run_bass_kernel_spmd(get_prog("a"), [dict(common, x=xs[c]) for c in cores], core_ids=cores).results
        kTa_all = to_global_cols([ra[c]["o_kTa"] for c in cores], 256)
        va_all = to_global_rows([ra[c]["o_va"] for c in cores], 256)
        kTb_all = to_global_cols([ra[c]["o_kTb"] for c in cores], 512)
        vb_all = to_global_rows([ra[c]["o_vb"] for c in cores], 512)
        hcT_all = to_global_cols([ra[c]["o_hcT"] for c in cores], 256)
        kTb_pad = np.concatenate([np.zeros((512, 512), bf), kTb_all], axis=1)
        vb_aug = np.zeros((S_TOT + 512, 8, 65), bf)
        vb_aug[512:, :, :64] = vb_all.reshape(S_TOT, 8, 64)
        vb_aug[512:, :, 64] = 1.0
        vb_aug = vb_aug.reshape(S_TOT + 512, 520)
        hc_pad = np.concatenate([np.zeros((256, 2), f32), hcT_all], axis=1)
        biasT = bias_table(P["rel_bias"][l])
        cw = np.ascontiguousarray(P["conv_w"][l].T.reshape(2, 128, 3).transpose(1, 0, 2).reshape(128, 6))
        onw = np.ascontiguousarray(np.broadcast_to(P["out_norm_w"][l][None, :], (128, 1024)))
        common = {"kTa_all": kTa_all, "va_all": va_all, "biasT": biasT, "cw": cw, "Ap": Ap, "identf": identf,
                  "onw": onw, "w_out": np.ascontiguousarray(P["w_out"][l])}
        maps = []
        for c in cores:
            gbs = [8 * i + c for i in range(NB)]
            kTbL = np.stack([kTb_pad[:, g * 128:g * 128 + 640] for g in gbs])
            vbL = np.stack([vb_aug[g * 128:g * 128 + 640] for g in gbs])
            hcL = np.stack([hc_pad[:, g * 128:g * 128 + 130] for g in gbs], axis=1)
            maps.append(dict(common, x=xs[c], qTa=ra[c]["o_qTa"], qTb=ra[c]["o_qTb"], kTbL=np.ascontiguousarray(kTbL),
                             vbL=np.ascontiguousarray(vbL), hcL=np.ascontiguousarray(hcL), gb=ra[c]["o_gb"],
                             mask=masks[c]))
        rb1 = run_bass_kernel_spmd(get_prog("b1"), maps, core_ids=cores).results
        if _debug is not None:
            _debug["b1_%d" % l] = [np.asarray(rb1[c]["xo"]) for c in cores]
        common = {"fnwT": np.ascontiguousarray(P["ffn_norm_w"][l].reshape(8, 128).T), "identf": identf,
                  "w_gate": np.ascontiguousarray(P["w_gate"][l]), "w_up": np.ascontiguousarray(P["w_up"][l]),
                  "w_down": np.ascontiguousarray(P["w_down"][l])}
        rb2 = run_bass_kernel_spmd(get_prog("b2"), [dict(common, x=np.asarray(rb1[c]["xo"])) for c in cores],
                                   core_ids=cores).results
        xs = [np.asarray(rb2[c]["xo"]) for c in cores]
    out = np.zeros((S_TOT, 1024), f32)
    for c in cores:
        out[toks[c]] = xs[c]
    return out[None]
```
